# Optimizing a Trainium2 kernel written in Bass

```python
import jax, jax.numpy as jnp
from jax import lax
import numpy as np

D_MODEL = 1024
BATCH = 8
SEQ = 4096
DEPTH = 1

D_FF = 2816
D_PLE = 256
D_GMLP = D_MODEL
N_SGU_GROUPS = 4
CHUNK = 128
D_POOL = D_MODEL
POOL_WINDOWS = (2, 4, 8, 16)
N_POOL_GROUPS = len(POOL_WINDOWS)
D_IN = 2 * D_GMLP + D_POOL + 2 * D_MODEL
EPS = 1e-6

kernel_name = "hybrid_sgu_pool_macaron_layer"


def _rmsnorm(x, g):
    xf = x.astype(jnp.float32)
    y = xf * lax.rsqrt(jnp.mean(xf * xf, axis=-1, keepdims=True) + EPS)
    return (y * g.astype(jnp.float32)).astype(x.dtype)


def _layernorm(x, g):
    xf = x.astype(jnp.float32)
    mu = jnp.mean(xf, axis=-1, keepdims=True)
    xc = xf - mu
    y = xc * lax.rsqrt(jnp.mean(xc * xc, axis=-1, keepdims=True) + EPS)
    return (y * g.astype(jnp.float32)).astype(x.dtype)


def _swiglu(xn, w_gate, w_up, w_down):
    return (jax.nn.silu(xn @ w_gate) * (xn @ w_up)) @ w_down


def _spatial_gating(u, v, norm_g, w_s, b_s):
    B, S, _ = v.shape
    dg = D_GMLP // N_SGU_GROUPS
    v = _layernorm(v, norm_g)
    vc = v.reshape(B, S // CHUNK, CHUNK, N_SGU_GROUPS, dg)
    causal = jnp.tril(jnp.ones((CHUNK, CHUNK), dtype=bool))
    ws = jnp.where(causal[None], w_s, 0.0).astype(v.dtype)
    sv = jnp.einsum('gts,bcsgd->bctgd', ws, vc) + b_s.T[:, :, None].astype(v.dtype)
    return u * sv.reshape(B, S, D_GMLP)


def _pool_mixer(c, pool_w, pool_scale):
    B, S, _ = c.shape
    dg = D_POOL // N_POOL_GROUPS
    cf = c.astype(jnp.float32)
    cs = jnp.concatenate([jnp.zeros((B, 1, D_POOL), jnp.float32), jnp.cumsum(cf, axis=1)], axis=1)
    t = jnp.arange(S)
    outs = []
    for gi, w in enumerate(POOL_WINDOWS):
        lo = jnp.maximum(t + 1 - w, 0)
        sl = slice(gi * dg, (gi + 1) * dg)
        csg = cs[:, :, sl]
        count = (t + 1 - lo).astype(jnp.float32)[None, :, None]
        mean = (csg[:, 1:] - csg[:, lo]) / count
        diff = (mean - cf[:, :, sl]).astype(c.dtype)
        outs.append(jnp.einsum('bsc,cd->bsd', diff, pool_w[gi]))
    return jnp.concatenate(outs, axis=-1) * pool_scale


def _token_mixer(xn, w_in, sgu_norm_g, sgu_w, sgu_b, pool_w, pool_scale, w_out_a, w_out_b, w_o):
    z = xn @ w_in
    i1 = D_GMLP
    i2 = 2 * D_GMLP
    i3 = i2 + D_POOL
    i4 = i3 + D_MODEL
    u = jax.nn.gelu(z[..., :i1])
    v = jax.nn.gelu(z[..., i1:i2])
    c = z[..., i2:i3]
    ga = z[..., i3:i4]
    gb = z[..., i4:]
    a = _spatial_gating(u, v, sgu_norm_g, sgu_w, sgu_b)
    b = _pool_mixer(c, pool_w, pool_scale)
    y = jax.nn.sigmoid(ga) * (a @ w_out_a) + jax.nn.sigmoid(gb) * (b @ w_out_b)
    return y @ w_o


def _gain(k, shape):
    return 1.0 + 0.02 * jax.random.normal(k, shape, jnp.float32)


def _w(k, shape, fan_in):
    return jax.random.normal(k, shape, jnp.float32) * (fan_in ** -0.5)


def setup_inputs(seed: int = 0) -> dict:
    key = jax.random.key(seed)
    ks = jax.random.split(key, 32)
    L = DEPTH
    dgp = D_POOL // N_POOL_GROUPS
    return {
        "x": jax.random.normal(ks[0], (BATCH, SEQ, D_MODEL), jnp.float32),
        "p": jax.random.normal(ks[1], (DEPTH, BATCH, SEQ, D_PLE), jnp.float32),
        "ffn1_pre_g": _gain(ks[2], (L, D_MODEL)),
        "ffn1_w_gate": _w(ks[3], (L, D_MODEL, D_FF), D_MODEL),
        "ffn1_w_up": _w(ks[4], (L, D_MODEL, D_FF), D_MODEL),
        "ffn1_w_down": _w(ks[5], (L, D_FF, D_MODEL), D_FF),
        "ffn1_post_g": _gain(ks[6], (L, D_MODEL)),
        "mix_pre_g": _gain(ks[7], (L, D_MODEL)),
        "w_in": _w(ks[8], (L, D_MODEL, D_IN), D_MODEL),
        "sgu_norm_g": _gain(ks[9], (L, D_GMLP)),
        "sgu_w": _w(ks[10], (L, N_SGU_GROUPS, CHUNK, CHUNK), CHUNK),
        "sgu_b": _gain(ks[11], (L, N_SGU_GROUPS, CHUNK)),
        "pool_w": _w(ks[12], (L, N_POOL_GROUPS, dgp, dgp), dgp),
        "pool_scale": _gain(ks[13], (L, D_POOL)),
        "w_out_a": _w(ks[14], (L, D_GMLP, D_MODEL), D_GMLP),
        "w_out_b": _w(ks[15], (L, D_POOL, D_MODEL), D_POOL),
        "w_o": _w(ks[16], (L, D_MODEL, D_MODEL), D_MODEL),
        "mix_post_g": _gain(ks[17], (L, D_MODEL)),
        "ffn2_pre_g": _gain(ks[18], (L, D_MODEL)),
        "ffn2_w_gate": _w(ks[19], (L, D_MODEL, D_FF), D_MODEL),
        "ffn2_w_up": _w(ks[20], (L, D_MODEL, D_FF), D_MODEL),
        "ffn2_w_down": _w(ks[21], (L, D_FF, D_MODEL), D_FF),
        "ffn2_post_g": _gain(ks[22], (L, D_MODEL)),
        "ple_pre_g": _gain(ks[23], (L, D_MODEL)),
        "ple_w_gate": _w(ks[24], (L, D_MODEL, D_MODEL), D_MODEL),
        "ple_w_proj": _w(ks[25], (L, D_PLE, D_MODEL), D_PLE),
        "ple_post_g": _gain(ks[26], (L, D_MODEL)),
    }


def reference(x, p, ffn1_pre_g, ffn1_w_gate, ffn1_w_up, ffn1_w_down, ffn1_post_g,
              mix_pre_g, w_in, sgu_norm_g, sgu_w, sgu_b, pool_w, pool_scale,
              w_out_a, w_out_b, w_o, mix_post_g,
              ffn2_pre_g, ffn2_w_gate, ffn2_w_up, ffn2_w_down, ffn2_post_g,
              ple_pre_g, ple_w_gate, ple_w_proj, ple_post_g):
    h = x
    for i in range(DEPTH):
        f = _swiglu(_rmsnorm(h, ffn1_pre_g[i]), ffn1_w_gate[i], ffn1_w_up[i], ffn1_w_down[i])
        h = h + 0.5 * _rmsnorm(f, ffn1_post_g[i])
        m = _token_mixer(_rmsnorm(h, mix_pre_g[i]), w_in[i], sgu_norm_g[i], sgu_w[i], sgu_b[i],
                         pool_w[i], pool_scale[i], w_out_a[i], w_out_b[i], w_o[i])
        h = h + _rmsnorm(m, mix_post_g[i])
        f = _swiglu(_rmsnorm(h, ffn2_pre_g[i]), ffn2_w_gate[i], ffn2_w_up[i], ffn2_w_down[i])
        h = h + 0.5 * _rmsnorm(f, ffn2_post_g[i])
        gate = jax.nn.sigmoid(_rmsnorm(h, ple_pre_g[i]) @ ple_w_gate[i])
        e = p[i] @ ple_w_proj[i]
        h = h + _rmsnorm(gate * e, ple_post_g[i])
    return h
```

```python
import numpy as np
import concourse.bass as bass
import concourse.mybir as mybir
from concourse.bass_utils import run_bass_kernel_spmd

F32 = mybir.dt.float32
BF16 = mybir.dt.bfloat16
I32 = mybir.dt.int32
AF = mybir.ActivationFunctionType
ALU = mybir.AluOpType

S_LEN = 4096
D = 1024
DFF = 2816
NJ = DFF // 128
DPLE = 256
T = 512
NT = S_LEN // T
KD = D // 128
EPS = 1e-6
NSLOT = 4
LOADW = 4096


class Buf:
    __slots__ = ("name", "w", "rs", "excl")

    def __init__(self, name, excl=False):
        self.name = name
        self.w = None
        self.rs = []
        self.excl = excl


class _Op:
    __slots__ = ("eng", "fn", "deps", "kind", "signal", "token")


class Sched:
    ENG = ("pe", "act", "dve", "pool", "sp")

    def __init__(self, nc, n_dma_sems=12):
        self.nc = nc
        self.h = {"pe": nc.tensor, "act": nc.scalar, "dve": nc.vector, "pool": nc.gpsimd, "sp": nc.sync}
        self.ops = []
        self.n_dma_sems = n_dma_sems

    def op(self, eng, fn, reads=(), writes=(), kind="c"):
        n = len(self.ops)
        deps = set()
        if any(r.excl for r in reads):
            writes = list(writes) + [r for r in reads if r.excl and r not in writes]
            reads = [r for r in reads if not r.excl]
        for r in reads:
            if r.w is not None:
                deps.add(r.w)
        for w in writes:
            if w.w is not None:
                deps.add(w.w)
            deps.update(w.rs)
        for r in reads:
            r.rs.append(n)
        for w in writes:
            w.w = n
            w.rs = []
        o = _Op()
        o.eng = eng
        o.fn = fn
        o.kind = kind
        o.signal = False
        o.token = None
        best = {}
        dl = []
        for d in deps:
            od = self.ops[d]
            if od.kind == "d":
                dl.append(d)
                continue
            if od.eng == "pe" and eng == "pe" and kind == "c":
                continue
            if od.eng not in best or best[od.eng] < d:
                best[od.eng] = d
        dl.extend(best.values())
        for d in dl:
            self.ops[d].signal = True
        o.deps = sorted(dl)
        self.ops.append(o)
        return n

    def dma(self, eng, fn, reads=(), writes=()):
        return self.op(eng, fn, reads, writes, kind="d")

    def emit(self):
        nc = self.nc
        esem = {e: nc.alloc_semaphore("prog_" + e) for e in ("pe", "act", "dve", "pool")}
        dsem = {e: [nc.alloc_semaphore(f"dma_{e}_{i}") for i in range(self.n_dma_sems)] for e in ("sp", "pool", "act")}
        dcnt = {e: [0] * self.n_dma_sems for e in ("sp", "pool", "act")}
        drr = {e: 0 for e in ("sp", "pool", "act")}
        cnt = {e: 0 for e in esem}
        seen = {e: {} for e in self.ENG}
        nwait = 0
        for o in self.ops:
            e = self.h[o.eng]
            sn = seen[o.eng]
            for d in o.deps:
                sem, val = self.ops[d].token
                key = id(sem)
                if sn.get(key, 0) < val:
                    e.wait_ge(sem, val)
                    sn[key] = val
                    nwait += 1
            if o.kind == "d":
                k = drr[o.eng] % self.n_dma_sems
                drr[o.eng] += 1
                sem = dsem[o.eng][k]
                prev = dcnt[o.eng][k]
                if prev > 0 and sn.get(id(sem), 0) < prev:
                    e.wait_ge(sem, prev)
                    sn[id(sem)] = prev
                    nwait += 1
                ins = o.fn(e)
                ins.then_inc(sem, 16)
                dcnt[o.eng][k] = prev + 16
                o.token = (sem, prev + 16)
            else:
                ins = o.fn(e)
                if o.signal:
                    cnt[o.eng] += 1
                    ins.then_inc(esem[o.eng], 1)
                    o.token = (esem[o.eng], cnt[o.eng])
        for q in dsem:
            for k in range(self.n_dma_sems):
                if dcnt[q][k] > 0 and seen[q].get(id(dsem[q][k]), 0) < dcnt[q][k]:
                    self.h[q].wait_ge(dsem[q][k], dcnt[q][k])
        self.stats = dict(n_ops=len(self.ops), n_wait=nwait, cnt=dict(cnt))

    def mm(self, out, lhsT, rhs, start, stop, reads, writes):
        return self.op("pe", lambda e: e.matmul(out, lhsT, rhs, start=start, stop=stop), reads, writes)

    def tr(self, out, in_, ident, reads, writes):
        return self.op("pe", lambda e: e.transpose(out, in_, ident), reads, writes)

    def act(self, out, in_, func, reads, writes, scale=None, bias=None, accum=None):
        kw = {}
        if scale is not None:
            kw["scale"] = scale
        if bias is not None:
            kw["bias"] = bias
        if accum is not None:
            kw["accum_out"] = accum
        return self.op("act", lambda e: e.activation(out=out, in_=in_, func=func, **kw), reads, writes)

    def tsc(self, eng, out, in0, s1, s2, op0, op1, reads, writes):
        if op1 is None:
            return self.op(eng, lambda e: e.tensor_scalar(out=out, in0=in0, scalar1=s1, scalar2=None, op0=op0), reads, writes)
        return self.op(eng, lambda e: e.tensor_scalar(out=out, in0=in0, scalar1=s1, scalar2=s2, op0=op0, op1=op1), reads, writes)

    def tt(self, eng, out, in0, in1, op, reads, writes):
        return self.op(eng, lambda e: e.tensor_tensor(out=out, in0=in0, in1=in1, op=op), reads, writes)

    def stt(self, out, in0, scalar, in1, op0, op1, reads, writes):
        return self.op("dve", lambda e: e.scalar_tensor_tensor(out=out, in0=in0, scalar=scalar, in1=in1, op0=op0, op1=op1), reads, writes)

    def copy(self, eng, out, in_, reads, writes):
        if eng == "act":
            return self.act(out, in_, AF.Copy, reads, writes)
        return self.op(eng, lambda e: e.tensor_copy(out=out, in_=in_), reads, writes)


class StatRing:
    NCOL = 64

    def __init__(self, nc, nsets=12):
        self.t = nc.alloc_sbuf_tensor("statring", [128, nsets * self.NCOL], F32).ap()
        self.nsets = nsets
        self.bufs = [[Buf(f"st{s}_{c}") for c in range(self.NCOL)] for s in range(nsets)]
        self.i = 0

    def new(self):
        s = self.i % self.nsets
        self.i += 1
        return _StatSet(self.t, s * self.NCOL, self.bufs[s])


class _StatSet:
    def __init__(self, t, base, bufs):
        self.t = t
        self.base = base
        self.bufs = bufs

    def col(self, c, n=1):
        return self.t[:, self.base + c:self.base + c + n]

    def b(self, c):
        return self.bufs[c]


NEWTON_STEPS = 2
INV_SQRT_D = 1.0 / 32.0


def rsqrt_chain(S, st, make_x, w=1):
    X, Y, Y2, TT = 8, 12, 16, 20
    x = st.col(X, w)
    y = st.col(Y, w)
    y2 = st.col(Y2, w)
    t = st.col(TT, w)
    make_x(x, st.b(X))
    S.tsc("dve", y.bitcast(I32), x.bitcast(I32), -0.5, float(0x5F3759DF), ALU.mult, ALU.add, [st.b(X)], [st.b(Y)])
    for _ in range(NEWTON_STEPS):
        S.tt("dve", y2, y, y, ALU.mult, [st.b(Y)], [st.b(Y2)])
        S.stt(t, y2, -0.5, x, ALU.mult, ALU.mult, [st.b(Y2), st.b(X)], [st.b(TT)])
        S.stt(y, t, 1.5, y, ALU.add, ALU.mult, [st.b(TT), st.b(Y)], [st.b(Y)])
    return y, st.b(Y)


def _lhsT_blocks(W):
    K, M = W.shape
    return W.reshape(K // 128, 128, M // 128, 128).transpose(1, 2, 0, 3)


def _rhs_blocks(W):
    K, N = W.shape
    return W.reshape(K // 128, 128, N // 512, 512).transpose(1, 2, 0, 3)


def stream_plan():
    blocks = []

    def ffn(tag, ph):
        for j in range(NJ):
            blocks.append((f"{tag}_gu{j}", 2048, ph))
        for nh in range(2):
            for j in range(NJ):
                blocks.append((f"{tag}_d{nh}_{j}", 512, ph))

    ffn("f1", 0)
    for nh in range(2):
        blocks.append((f"mc{nh}", 4096, 1))
    for nh in range(2):
        blocks.append((f"mv{nh}", 4096, 1))
    for mc in range(8):
        blocks.append((f"mu{mc}", 1024, 1))
    blocks.append(("poolw", 2048, 1))
    for mc in range(8):
        blocks.append((f"gated{mc}", 4096, 1))
    for nh in range(2):
        blocks.append((f"wo{nh}", 4096, 1))
    ffn("f2", 2)
    for nh in range(2):
        blocks.append((f"pleg{nh}", 4096, 3))
        blocks.append((f"plep{nh}", 1024, 3))
    loads = []
    where = {}
    off = 0
    cur = None
    for name, sz, ph in blocks:
        if cur is None or cur[1] + sz > LOADW or cur[2] != ph:
            cur = [off, 0, ph]
            loads.append(cur)
        where[name] = (len(loads) - 1, cur[1], sz)
        cur[1] += sz
        off += sz
    return [(n, z) for n, z, _ in blocks], loads, where, off


def build_wsrc(inp):
    parts = {}

    def ffn(tag, wg, wu, wd):
        g = _lhsT_blocks(wg)
        u = _lhsT_blocks(wu)
        for j in range(NJ):
            parts[f"{tag}_gu{j}"] = np.concatenate([g[:, j].reshape(128, 1024), u[:, j].reshape(128, 1024)], axis=1)
        d = wd.reshape(NJ, 128, 2, 512)
        for nh in range(2):
            for j in range(NJ):
                parts[f"{tag}_d{nh}_{j}"] = d[j, :, nh, :]

    ffn("f1", inp["ffn1_w_gate"][0], inp["ffn1_w_up"][0], inp["ffn1_w_down"][0])
    ffn("f2", inp["ffn2_w_gate"][0], inp["ffn2_w_up"][0], inp["ffn2_w_down"][0])
    w_in = inp["w_in"][0]
    v = _rhs_blocks(w_in[:, 1024:2048])
    c = _rhs_blocks(w_in[:, 2048:3072])
    for nh in range(2):
        parts[f"mv{nh}"] = v[:, nh].reshape(128, 4096)
        parts[f"mc{nh}"] = c[:, nh].reshape(128, 4096)
    u = _lhsT_blocks(w_in[:, 0:1024])
    ga = _lhsT_blocks(w_in[:, 3072:4096])
    gb = _lhsT_blocks(w_in[:, 4096:5120])
    oa = _lhsT_blocks(inp["w_out_a"][0])
    ob = _lhsT_blocks(inp["w_out_b"][0])
    for mc in range(8):
        parts[f"mu{mc}"] = u[:, mc].reshape(128, 1024)
        parts[f"gated{mc}"] = np.concatenate(
            [ga[:, mc].reshape(128, 1024), oa[:, mc].reshape(128, 1024), gb[:, mc].reshape(128, 1024), ob[:, mc].reshape(128, 1024)], axis=1)
    pw = inp["pool_w"][0]
    parts["poolw"] = np.concatenate([pw[g].reshape(2, 128, 256).transpose(1, 0, 2).reshape(128, 512) for g in range(4)], axis=1)
    wo = _rhs_blocks(inp["w_o"][0])
    pg = _rhs_blocks(inp["ple_w_gate"][0])
    pp = _rhs_blocks(inp["ple_w_proj"][0])
    for nh in range(2):
        parts[f"wo{nh}"] = wo[:, nh].reshape(128, 4096)
        parts[f"pleg{nh}"] = pg[:, nh].reshape(128, 4096)
        parts[f"plep{nh}"] = pp[:, nh].reshape(128, 1024)
    blocks, loads, where, tot = stream_plan()
    out = np.empty((128, tot), np.float32)
    off = 0
    for name, sz in blocks:
        a = parts[name]
        assert a.shape == (128, sz), (name, a.shape, sz)
        out[:, off:off + sz] = a
        off += sz
    return out


C_GB = 0
C_GCOL = 5120
C_PSCOL = 5152
C_PERS = 5160
C_WST = 5160
C_MASK = 5672
C_BROW = 5800
C_P = 6312
C_ID = 8360
C_END = 8488
POOL_WINDOWS = (2, 4, 8, 16)


def build_consts(inp):
    import ml_dtypes
    cst = np.zeros((128, C_END), np.float32)
    for i, k in enumerate(["ffn1_post_g", "mix_post_g", "ffn2_post_g", "ple_post_g", "sgu_norm_g"]):
        cst[:, C_GB + i * 1024:C_GB + (i + 1) * 1024] = inp[k][0][None, :]
    for i, k in enumerate(["ffn1_pre_g", "mix_pre_g", "ffn2_pre_g", "ple_pre_g"]):
        cst[:, C_GCOL + i * 8:C_GCOL + (i + 1) * 8] = inp[k][0].reshape(8, 128).T
    cst[:, C_PSCOL:C_PSCOL + 8] = inp["pool_scale"][0].reshape(8, 128).T
    for g in range(4):
        cst[:, C_WST + g * 128:C_WST + (g + 1) * 128] = inp["sgu_w"][0][g].T
        cst[:, C_BROW + g * 128:C_BROW + (g + 1) * 128] = inp["sgu_b"][0][g][None, :]
    s = np.arange(128)[:, None]
    t = np.arange(128)[None, :]
    cst[:, C_MASK:C_MASK + 128] = (s <= t).astype(np.float32)
    for wi, w in enumerate(POOL_WINDOWS):
        cur = np.where((t - s >= 0) & (t - s < w), 1.0 / w, 0.0) - (s == t)
        prev = np.where((t + 128 - s) < w, 1.0 / w, 0.0)
        cnt = np.minimum(t + 1, w).astype(np.float64)
        first = np.where((t - s >= 0) & (t - s < w), 1.0 / cnt, 0.0) - (s == t)
        hi = first.astype(np.float32).astype(ml_dtypes.bfloat16).astype(np.float32)
        lo = (first - hi).astype(np.float32).astype(ml_dtypes.bfloat16).astype(np.float32)
        cst[:, C_P + wi * 128:C_P + (wi + 1) * 128] = cur
        cst[:, C_P + (4 + wi) * 128:C_P + (5 + wi) * 128] = prev
        cst[:, C_P + (8 + wi) * 128:C_P + (9 + wi) * 128] = hi
        cst[:, C_P + (12 + wi) * 128:C_P + (13 + wi) * 128] = lo
    cst[:, C_ID:C_ID + 128] = np.eye(128, dtype=np.float32)
    return cst


class Ctx:
    pass


def build_program(n_tiles=NT, phases=4):
    nc = bass.Bass("TRN2", target_bir_lowering=False)
    S = Sched(nc)
    blocks, loads, where, WTOT = stream_plan()
    NL = len(loads)

    x_d = nc.dram_tensor("x", [S_LEN, D], F32, kind="ExternalInput").ap()
    p_d = nc.dram_tensor("p", [S_LEN, DPLE], F32, kind="ExternalInput").ap()
    w_d = nc.dram_tensor("wsrc", [128, WTOT], F32, kind="ExternalInput").ap()
    c_d = nc.dram_tensor("consts", [128, C_END], F32, kind="ExternalInput").ap()
    o_d = nc.dram_tensor("out", [S_LEN, D], F32, kind="ExternalOutput").ap()
    wbf_d = nc.dram_tensor("wbf", [128, WTOT], BF16, kind="Internal").ap()
    wbfB = [Buf(f"wbf{l}") for l in range(NL)]

    def sb(name, shape, dt):
        return nc.alloc_sbuf_tensor(name, shape, dt).ap()

    NH = 3
    h = [sb(f"h{i}", [128, 4, D], F32) for i in range(NH)]
    hB = [[Buf(f"h{i}_{c}") for c in range(4)] for i in range(NH)]
    slots = [sb(f"wslot{i}", [128, LOADW], BF16) for i in range(NSLOT)]
    slotB = [Buf(f"wslot{i}") for i in range(NSLOT)]
    xntok = sb("xntok", [128, 4, D], BF16)
    xntokB = [[Buf(f"xntok{c}_{hf}") for hf in range(2)] for c in range(4)]
    xnT = sb("xnT", [128, KD, T], BF16)
    xnTB = [Buf(f"xnT{k}") for k in range(KD)]
    NFM = 24
    fm = sb("fm", [128, NFM, T], BF16)
    fmB = [Buf(f"fm{i}") for i in range(NFM)]
    tmp = sb("tmp", [128, 4, D], F32)
    tmpB = [[Buf(f"tmp{c}_{nh}") for nh in range(2)] for c in range(4)]
    vn = sb("vn", [128, 4, D], BF16)
    vnB = [Buf(f"vn{c}") for c in range(4)]
    NCT = 6
    ctok = sb("ctok", [128, NCT, D], BF16)
    ctokB = [Buf(f"ctok{i}") for i in range(NCT)]
    NEW = 5
    ew = [sb(f"ew{i}", [128, T], F32) for i in range(NEW)]
    ewB = [Buf(f"ew{i}") for i in range(NEW)]
    junk = sb("junk", [128, D], BF16)
    junkB = Buf("junk")
    cpers = sb("cpers", [128, C_PERS], F32)
    cpersB = Buf("cpers_gb")
    ccolB = Buf("cpers_cols")
    wst = sb("wst", [128, 512], BF16)
    pmat = sb("pmat", [128, 16 * 128], BF16)
    ident = sb("ident", [128, 128], BF16)
    bhi = sb("bhi", [1, 512], BF16)
    blo = sb("blo", [1, 512], BF16)
    bres = sb("bres", [1, 512], F32)
    ones = sb("ones", [2, 128], BF16)
    bbc = sb("bbc", [128, 512], F32)
    cbfB = Buf("cbf")
    pbf = sb("pbf", [128, 4, DPLE], BF16)
    pbfB = Buf("pbf")
    pT = sb("pT", [128, 2 * T], BF16)
    pTB = Buf("pT")
    sr = StatRing(nc, nsets=12)

    ps = [nc.alloc_psum_tensor(f"ps{i}", [128, 512], F32).ap() for i in range(8)]
    psB = [Buf(f"ps{i}", excl=True) for i in range(8)]
    cx = Ctx()
    cx.bank = 0
    cx.ew = 0
    cx.flip = 0
    cx.pending = None

    def newbank():
        b = cx.bank % 8
        cx.bank += 1
        return b

    def newew():
        i = cx.ew % NEW
        cx.ew += 1
        return i

    def flip():
        cx.flip ^= 1
        return "act" if cx.flip else "dve"

    GB = lambda i: cpers[:, C_GB + i * 1024:C_GB + (i + 1) * 1024]
    GB_F1, GB_MIX, GB_F2, GB_PLE, GB_SGU = range(5)

    stage = tmp.rearrange("p a b -> p (a b)")
    allTmp = [b for row in tmpB for b in row]
    so = lambda off: off - C_PERS
    S.dma("sp", lambda e: e.dma_start(out=stage[:, so(C_ID):so(C_ID) + 128], in_=c_d[:, C_ID:C_ID + 128]), [], allTmp)
    S.dma("sp", lambda e: e.dma_start(out=cpers[:, C_GCOL:C_PERS], in_=c_d[:, C_GCOL:C_PERS]), [], [ccolB])
    S.copy("dve", ident[:, :], stage[:, so(C_ID):so(C_ID) + 128], allTmp, [cbfB])

    def load_bulk_consts():
        S.dma("sp", lambda e: e.dma_start(out=cpers[:, 0:C_GCOL], in_=c_d[:, 0:C_GCOL]), [hB[0][0]], [cpersB])
        S.dma("sp", lambda e: e.dma_start(out=stage[:, 0:so(C_ID)], in_=c_d[:, C_PERS:C_ID]), [hB[0][0]], allTmp)

    def setup_rest():
        for g in range(4):
            (lambda g: S.tt("dve", wst[:, g * 128:(g + 1) * 128], stage[:, so(C_WST) + g * 128:so(C_WST) + (g + 1) * 128],
                            stage[:, so(C_MASK):so(C_MASK) + 128], ALU.mult, allTmp, [cbfB]))(g)
        S.copy("dve", pmat[:, :], stage[:, so(C_P):so(C_P) + 2048], allTmp, [cbfB])
        S.copy("dve", bhi[0:1, :], stage[0:1, so(C_BROW):so(C_BROW) + 512], allTmp, [cbfB])
        S.tt("dve", bres[0:1, :], stage[0:1, so(C_BROW):so(C_BROW) + 512], bhi[0:1, :], ALU.subtract, allTmp + [cbfB], [cbfB])
        S.copy("dve", blo[0:1, :], bres[0:1, :], [cbfB], [cbfB])
        S.op("dve", lambda e: e.memset(ones[:, :], 1.0), [], [cbfB])
        for gi in (GB_F1, GB_F2):
            (lambda gi: S.tsc("dve", GB(gi), GB(gi), 0.5, None, ALU.mult, None, [cpersB], [cpersB]))(gi)

    laneA = [(t, ph) for t in range(0, n_tiles, 2) for ph in range(phases)]
    laneB = [(t, ph) for t in range(1, n_tiles, 2) for ph in range(phases)]
    nfill = min(3, phases)
    steps = []
    for k in range(nfill):
        steps.append(laneA[k])
        if k < len(laneB):
            steps.append(laneB[k])
    ia, ib = nfill, min(nfill, len(laneB))
    if ia < len(laneA):
        steps.append(laneA[ia])
        ia += 1
    while ia < len(laneA) or ib < len(laneB):
        if ia < len(laneA):
            steps.append(laneA[ia])
            ia += 1
        if ib < len(laneB):
            steps.append(laneB[ib])
            ib += 1
    gl_order = [(t, l) for (t, ph) in steps for l in range(NL) if loads[l][2] == ph]
    gl_pos = {tl: i for i, tl in enumerate(gl_order)}

    cx.next_load = 0
    cx.max_acc = -1

    def issue_loads(upto):
        while cx.next_load <= upto and cx.next_load < len(gl_order):
            G = cx.next_load
            t, l = gl_order[G]
            off, sz, _ = loads[l]
            s = G % NSLOT
            if t == 0:
                (lambda s, off, sz: S.dma("pool", lambda e: e.dma_start(out=slots[s][:, 0:sz], in_=w_d[:, off:off + sz],
                                                                      max_dma_last_dim=8192), [], [slotB[s]]))(s, off, sz)
                if n_tiles > 1:
                    (lambda s, off, sz, l: S.dma("sp", lambda e: e.dma_start(out=wbf_d[:, off:off + sz], in_=slots[s][:, 0:sz]),
                                                 [slotB[s]], [wbfB[l]]))(s, off, sz, l)
            else:
                (lambda s, off, sz, l: S.dma("sp", lambda e: e.dma_start(out=slots[s][:, 0:sz], in_=wbf_d[:, off:off + sz]),
                                             [wbfB[l]], [slotB[s]]))(s, off, sz, l)
            cx.next_load += 1

    def W(tile, name):
        l, o, sz = where[name]
        G = gl_pos[(tile, l)]
        assert G >= cx.max_acc - 1, (name, G, cx.max_acc)
        if G > cx.max_acc:
            cx.max_acc = G
            issue_loads(G + NSLOT - 2)
        s = G % NSLOT
        return slots[s][:, o:o + sz], slotB[s]

    def prenorm_squares(hp):
        st = sr.new()
        for c in range(4):
            S.act(junk[:, :], h[hp][:, c, :], AF.Square, [hB[hp][c]], [junkB, st.b(c)], accum=st.col(c), scale=INV_SQRT_D)
        return st

    def prenorm_stats(hp, defer=False):
        st = prenorm_squares(hp)
        if defer:
            cx.pending = lambda: prenorm_finish(hp, st)
            return
        prenorm_finish(hp, st)

    def prenorm_finish(hp, st):
        r, rB = rsqrt_chain(S, st, lambda x, xb: S.tsc("dve", x, st.col(0, 4), EPS, None, ALU.add, None,
                                                       [st.b(c) for c in range(4)], [xb]), w=4)
        for hf in range(2):
            cs = slice(hf * 512, (hf + 1) * 512)
            for c in range(4):
                if c == 3:
                    S.act(xntok[:, c, cs], h[hp][:, c, cs], AF.Copy, [hB[hp][c], rB], [xntokB[c][hf]], scale=r[:, c:c + 1])
                else:
                    S.tsc("dve", xntok[:, c, cs], h[hp][:, c, cs], r[:, c:c + 1], None, ALU.mult, None, [hB[hp][c], rB], [xntokB[c][hf]])

    def prenorm(hp, gci, stats_done=False):
        if not stats_done:
            prenorm_stats(hp)
        for kc in range(KD):
            b = newbank()
            pb = ps[b].bitcast(BF16)
            for c in range(4):
                S.tr(pb[:, c * 128:(c + 1) * 128], xntok[:, c, kc * 128:(kc + 1) * 128], ident[:, :],
                     [xntokB[c][kc // 4], cbfB], [psB[b]])
            gcol = cpers[:, C_GCOL + gci * 8 + kc:C_GCOL + gci * 8 + kc + 1]
            if flip() == "act":
                S.act(xnT[:, kc, :], pb[:, 0:T], AF.Copy, [psB[b], ccolB], [xnTB[kc]], scale=gcol)
            else:
                S.tsc("dve", xnT[:, kc, :], pb[:, 0:T], gcol, None, ALU.mult, None, [psB[b], ccolB], [xnTB[kc]])

    def postnorm_collect(b, c, nh, sts, gbi):
        S.act(junk[:, 0:T], ps[b][:, :], AF.Square, [psB[b]], [junkB, sts.b(nh * 4 + c)], accum=sts.col(nh * 4 + c), scale=INV_SQRT_D)
        S.tt("dve", tmp[:, c, nh * T:(nh + 1) * T], ps[b][:, :], GB(gbi)[:, nh * T:(nh + 1) * T], ALU.mult,
             [psB[b], cpersB], [tmpB[c][nh]])

    def residual_update(hp, st):
        r, rB = rsqrt_chain(S, st, lambda x, xb: S.stt(x, st.col(0, 4), EPS, st.col(4, 4), ALU.add, ALU.add,
                                                       [st.b(i) for i in range(8)], [xb]), w=4)
        for c in range(4):
            S.stt(h[hp][:, c, :], tmp[:, c, :], r[:, c:c + 1], h[hp][:, c, :], ALU.mult, ALU.add,
                  [tmpB[c][0], tmpB[c][1], rB, hB[hp][c]], [hB[hp][c]])

    def body_ffn(tile, tag, gbi, mid_hook=None):
        for j in range(NJ):
            wg, wB = W(tile, f"{tag}_gu{j}")
            bg = newbank()
            for kc in range(KD):
                S.mm(ps[bg][:, :], wg[:, kc * 128:(kc + 1) * 128], xnT[:, kc, :], kc == 0, kc == KD - 1, [wB, xnTB[kc]], [psB[bg]])
            bu = newbank()
            for kc in range(KD):
                S.mm(ps[bu][:, :], wg[:, 1024 + kc * 128:1024 + (kc + 1) * 128], xnT[:, kc, :], kc == 0, kc == KD - 1,
                     [wB, xnTB[kc]], [psB[bu]])
            e = newew()
            S.act(ew[e][:, :], ps[bg][:, :], AF.Silu, [psB[bg]], [ewB[e]])
            S.tt("dve", fm[:, j, :], ps[bu][:, :], ew[e][:, :], ALU.mult, [psB[bu], ewB[e]], [fmB[j]])
        if mid_hook is not None:
            mid_hook()
        sts = sr.new()
        for nh in range(2):
            banks = [newbank() for _ in range(4)]
            for j in range(NJ):
                wd, wB = W(tile, f"{tag}_d{nh}_{j}")
                for c in range(4):
                    S.mm(ps[banks[c]][:, :], fm[:, j, c * 128:(c + 1) * 128], wd, j == 0, j == NJ - 1, [fmB[j], wB], [psB[banks[c]]])
            for c in range(4):
                postnorm_collect(banks[c], c, nh, sts, gbi)
        return sts

    FU, FDIFF, FB, FY = 0, 8, 16, 8

    def bias_broadcast():
        bb = newbank()
        S.mm(ps[bb][:, :], ones[0:1, :], bhi[0:1, :], True, False, [cbfB], [psB[bb]])
        S.mm(ps[bb][:, :], ones[0:1, :], blo[0:1, :], False, True, [cbfB], [psB[bb]])
        S.copy("dve", bbc[:, :], ps[bb][:, :], [psB[bb]], [cbfB])

    cx.bias_done = False

    def body_mixer(tile):
        if not cx.bias_done:
            cx.bias_done = True
            bias_broadcast()
        cslot = lambda c: (tile * 4 + c) % NCT
        for nh in range(2):
            wc, wB = W(tile, f"mc{nh}")
            for c in range(4):
                b = newbank()
                for kc in range(KD):
                    S.mm(ps[b][:, :], xnT[:, kc, c * 128:(c + 1) * 128], wc[:, kc * T:(kc + 1) * T], kc == 0, kc == KD - 1,
                         [xnTB[kc], wB], [psB[b]])
                S.copy("act", ctok[:, cslot(c), nh * T:(nh + 1) * T], ps[b][:, :], [psB[b]], [ctokB[cslot(c)]])
        vst = sr.new()
        for nh in range(2):
            wv, wB = W(tile, f"mv{nh}")
            for c in range(4):
                b = newbank()
                for kc in range(KD):
                    S.mm(ps[b][:, :], xnT[:, kc, c * 128:(c + 1) * 128], wv[:, kc * T:(kc + 1) * T], kc == 0, kc == KD - 1,
                         [xnTB[kc], wB], [psB[b]])
                S.act(tmp[:, c, nh * T:(nh + 1) * T], ps[b][:, :], AF.Gelu_apprx_tanh, [psB[b]], [tmpB[c][nh]])
                (lambda c, nh: S.op("dve", lambda e: e.bn_stats(out=vst.col(c * 12 + nh * 6, 6), in_=tmp[:, c, nh * T:(nh + 1) * T]),
                                    [tmpB[c][nh]], [vst.b(c * 2 + nh)]))(c, nh)
        for c in range(4):
            (lambda c: S.op("dve", lambda e: e.bn_aggr(out=vst.col(48 + 2 * c, 2), in_=vst.col(c * 12, 12)),
                            [vst.b(c * 2), vst.b(c * 2 + 1)], [vst.b(48)]))(c)
        meanv = vst.t[:, vst.base + 48:vst.base + 56:2]
        varv = vst.t[:, vst.base + 49:vst.base + 56:2]
        r, rB = rsqrt_chain(S, sr.new(), lambda x, xb: S.tsc("dve", x, varv, EPS, None, ALU.add, None, [vst.b(48)], [xb]), w=4)
        S.stt(vst.col(56, 4), meanv, -1.0, r, ALU.mult, ALU.mult, [vst.b(48), rB], [vst.b(56)])
        for c in range(4):
            S.act(tmp[:, c, :], tmp[:, c, :], AF.Identity, [tmpB[c][0], tmpB[c][1], rB, vst.b(56)], [tmpB[c][0], tmpB[c][1]],
                  scale=r[:, c:c + 1], bias=vst.col(56 + c))
            S.tt("dve", vn[:, c, :], tmp[:, c, :], GB(GB_SGU), ALU.mult, [tmpB[c][0], tmpB[c][1], cpersB], [vnB[c]])
        for mc in range(8):
            wu, wB = W(tile, f"mu{mc}")
            b = newbank()
            for kc in range(KD):
                S.mm(ps[b][:, :], wu[:, kc * 128:(kc + 1) * 128], xnT[:, kc, :], kc == 0, kc == KD - 1, [wB, xnTB[kc]], [psB[b]])
            S.act(fm[:, FU + mc, :], ps[b][:, :], AF.Gelu_apprx_tanh, [psB[b]], [fmB[FU + mc]])
        for dc in range(8):
            g = dc // 2
            b = newbank()
            for c in range(4):
                o = ps[b][:, c * 128:(c + 1) * 128]
                S.mm(o, vn[:, c, dc * 128:(dc + 1) * 128], wst[:, g * 128:(g + 1) * 128], True, True, [vnB[c], cbfB], [psB[b]])
            e = newew()
            v3 = lambda ap: ap.rearrange("p (c t) -> p c t", c=4)
            bview = bbc[:, g * 128:(g + 1) * 128][:, None, :].broadcast_to([128, 4, 128])
            S.tt("dve", v3(ew[e][:, :]), v3(ps[b][:, :]), bview, ALU.add, [psB[b], cbfB], [ewB[e]])
            S.tt("dve", fm[:, FU + dc, :], ew[e][:, :], fm[:, FU + dc, :], ALU.mult, [ewB[e], fmB[FU + dc]], [fmB[FU + dc]])
        PM = lambda i: pmat[:, i * 128:(i + 1) * 128]
        for dc in range(8):
            wi = dc // 2
            b = newbank()
            for c in range(4):
                o = ps[b][:, c * 128:(c + 1) * 128]
                cur = ctok[:, cslot(c), dc * 128:(dc + 1) * 128]
                if tile == 0 and c == 0:
                    S.mm(o, cur, PM(8 + wi), True, False, [ctokB[cslot(c)], cbfB], [psB[b]])
                    S.mm(o, cur, PM(12 + wi), False, True, [ctokB[cslot(c)], cbfB], [psB[b]])
                else:
                    pslot = (tile * 4 + c - 1) % NCT
                    S.mm(o, cur, PM(wi), True, False, [ctokB[cslot(c)], cbfB], [psB[b]])
                    S.mm(o, ctok[:, pslot, dc * 128:(dc + 1) * 128], PM(4 + wi), False, True, [ctokB[pslot], cbfB], [psB[b]])
            S.copy(flip(), fm[:, FDIFF + dc, :], ps[b][:, :], [psB[b]], [fmB[FDIFF + dc]])
        wp, wB = W(tile, "poolw")
        for g in range(4):
            for oc in range(2):
                b = newbank()
                for k2 in range(2):
                    S.mm(ps[b][:, :], wp[:, g * 512 + k2 * 256 + oc * 128:g * 512 + k2 * 256 + (oc + 1) * 128], fm[:, FDIFF + 2 * g + k2, :],
                         k2 == 0, k2 == 1, [wB, fmB[FDIFF + 2 * g + k2]], [psB[b]])
                S.act(fm[:, FB + 2 * g + oc, :], ps[b][:, :], AF.Copy, [psB[b], ccolB], [fmB[FB + 2 * g + oc]],
                      scale=cpers[:, C_PSCOL + 2 * g + oc:C_PSCOL + 2 * g + oc + 1])
        for mc in range(8):
            wk, wB = W(tile, f"gated{mc}")
            res = []
            for half, src in ((0, FU), (1, FB)):
                bgt = newbank()
                for kc in range(KD):
                    S.mm(ps[bgt][:, :], wk[:, half * 2048 + kc * 128:half * 2048 + (kc + 1) * 128], xnT[:, kc, :], kc == 0, kc == KD - 1,
                         [wB, xnTB[kc]], [psB[bgt]])
                es = newew()
                S.act(ew[es][:, :], ps[bgt][:, :], AF.Sigmoid, [psB[bgt]], [ewB[es]])
                by = newbank()
                for kc in range(KD):
                    S.mm(ps[by][:, :], wk[:, half * 2048 + 1024 + kc * 128:half * 2048 + 1024 + (kc + 1) * 128], fm[:, src + kc, :],
                         kc == 0, kc == KD - 1, [wB, fmB[src + kc]], [psB[by]])
                S.tt("dve", ew[es][:, :], ps[by][:, :], ew[es][:, :], ALU.mult, [psB[by], ewB[es]], [ewB[es]])
                res.append(es)
            S.tt("dve", fm[:, FY + mc, :], ew[res[0]][:, :], ew[res[1]][:, :], ALU.add, [ewB[res[0]], ewB[res[1]]], [fmB[FY + mc]])
        sts = sr.new()
        for nh in range(2):
            wo, wB = W(tile, f"wo{nh}")
            for c in range(4):
                b = newbank()
                for kc in range(KD):
                    S.mm(ps[b][:, :], fm[:, FY + kc, c * 128:(c + 1) * 128], wo[:, kc * T:(kc + 1) * T], kc == 0, kc == KD - 1,
                         [fmB[FY + kc], wB], [psB[b]])
                postnorm_collect(b, c, nh, sts, GB_MIX)
        return sts

    def prep_p(tile):
        S.dma("pool", lambda e: e.dma_start(out=pbf[:, :, :], in_=p_d[tile * T:(tile + 1) * T, :].rearrange("(c q) d -> q c d", q=128)),
              [], [pbfB])
        b = newbank()
        pb = ps[b].bitcast(BF16)
        for k2 in range(2):
            for c in range(4):
                S.tr(pb[:, k2 * T + c * 128:k2 * T + (c + 1) * 128], pbf[:, c, k2 * 128:(k2 + 1) * 128], ident[:, :], [pbfB, cbfB], [psB[b]])
        S.copy("dve", pT[:, :], pb[:, 0:2 * T], [psB[b]], [pTB])

    def body_ple(tile):
        sts = sr.new()
        for nh in range(2):
            wg, wgB = W(tile, f"pleg{nh}")
            wp, wpB = W(tile, f"plep{nh}")
            for c in range(4):
                if nh == 0 and c == 2 and cx.pending is not None:
                    fin, cx.pending = cx.pending, None
                    fin()
                bg = newbank()
                for kc in range(KD):
                    S.mm(ps[bg][:, :], xnT[:, kc, c * 128:(c + 1) * 128], wg[:, kc * T:(kc + 1) * T], kc == 0, kc == KD - 1,
                         [xnTB[kc], wgB], [psB[bg]])
                be = newbank()
                for k2 in range(2):
                    S.mm(ps[be][:, :], pT[:, k2 * T + c * 128:k2 * T + (c + 1) * 128], wp[:, k2 * T:(k2 + 1) * T], k2 == 0, k2 == 1,
                         [pTB, wpB], [psB[be]])
                es = newew()
                S.act(ew[es][:, :], ps[bg][:, :], AF.Sigmoid, [psB[bg]], [ewB[es]])
                tv = tmp[:, c, nh * T:(nh + 1) * T]
                S.tt("dve", tv, ps[be][:, :], ew[es][:, :], ALU.mult, [psB[be], ewB[es]], [tmpB[c][nh]])
                S.act(junk[:, 0:T], tv, AF.Square, [tmpB[c][nh]], [junkB, sts.b(nh * 4 + c)], accum=sts.col(nh * 4 + c), scale=INV_SQRT_D)
                S.tt("dve", tv, tv, GB(GB_PLE)[:, nh * T:(nh + 1) * T], ALU.mult, [tmpB[c][nh], cpersB], [tmpB[c][nh]])
        return sts

    def load_x(tile, q="pool"):
        hp = tile % NH
        for c in range(4):
            r0 = tile * T + c * 128
            (lambda c, r0: S.dma(q, lambda e: e.dma_start(out=h[hp][:, c, :], in_=x_d[r0:r0 + 128, :]), [], [hB[hp][c]]))(c, r0)

    def store_h(tile):
        hp = tile % NH
        for c in range(4):
            r0 = tile * T + c * 128
            (lambda c, r0: S.dma("pool", lambda e: e.dma_start(out=o_d[r0:r0 + 128, :], in_=h[hp][:, c, :]), [hB[hp][c]], []))(c, r0)

    GCI = {0: 0, 1: 1, 2: 2, 3: 3}

    prep_at = {}
    last_ple = -1
    for q, (tq, phq) in enumerate(steps):
        if phq == 3:
            r = None
            for k in range(q - 1, last_ple, -1):
                if steps[k][1] in (0, 2):
                    r = k
                    break
            if r is not None:
                prep_at[r] = tq
            last_ple = q
    prepped = set()

    def BODY(st_, idx):
        t, ph = st_
        hooks = []
        if ph in (0, 2):
            if cx.first_hook:
                cx.first_hook = False
                hooks.append(first_hook)
            if idx == 1 and x2_late:
                hooks.append(lambda: load_x(2))
            if idx in prep_at and phases >= 4:
                tp = prep_at[idx]
                hooks.append(lambda: (prepped.add(tp), prep_p(tp)))
        hook = (lambda: [f() for f in hooks]) if hooks else None
        if ph == 0:
            return body_ffn(t, "f1", GB_F1, mid_hook=hook)
        if ph == 1:
            return body_mixer(t)
        if ph == 2:
            return body_ffn(t, "f2", GB_F2, mid_hook=hook)
        if t not in prepped:
            prepped.add(t)
            prep_p(t)
        return body_ple(t)

    def POST(st_, sts):
        t, ph = st_
        residual_update(t % NH, sts)
        if ph == phases - 1:
            store_h(t)
            if t + NH < n_tiles:
                load_x(t + NH)

    def PRE(st_, defer=False):
        prenorm_stats(st_[0] % NH, defer)

    def TR(st_):
        prenorm(0, GCI[st_[1]], stats_done=True)

    lane = lambda st_: st_[0] % 2
    load_x(0)
    issue_loads(NSLOT - 2)
    load_bulk_consts()
    if n_tiles > 1:
        load_x(1, q="sp")
    n = len(steps)
    cx.first_hook = True

    x2_late = n_tiles > 2 and len(steps) > 1 and steps[1][1] in (0, 2)

    def first_hook():
        setup_rest()
        if n > 1 and steps[1][0] != steps[0][0] and 1 not in pre_done:
            do_pre(1)
        if n_tiles > 2 and not x2_late:
            load_x(2)

    def prev_same_lane(i):
        for j in range(i - 1, -1, -1):
            if lane(steps[j]) == lane(steps[i]):
                return j
        return None

    pre_done = set()

    def do_pre(j):
        pre_done.add(j)
        defer = phases >= 4 and j >= 2 and steps[j - 1][1] == 3 and steps[j - 1][0] != steps[j][0]
        PRE(steps[j], defer)

    do_pre(0)
    TR(steps[0])
    for i in range(n):
        sts = BODY(steps[i], i)
        if i + 1 >= n:
            POST(steps[i], sts)
            break
        posted = False
        if lane(steps[i + 1]) == lane(steps[i]) and steps[i + 1][0] == steps[i][0]:
            POST(steps[i], sts)
            posted = True
        if (i + 1) not in pre_done:
            do_pre(i + 1)
        TR(steps[i + 1])
        if i + 2 < n and (i + 2) not in pre_done:
            p = prev_same_lane(i + 2)
            new_tile = p is None or steps[p][0] != steps[i + 2][0]
            if new_tile:
                do_pre(i + 2)
            elif p < i or (p == i and posted):
                do_pre(i + 2)
            elif p == i:
                POST(steps[i], sts)
                posted = True
                do_pre(i + 2)
        if not posted:
            POST(steps[i], sts)
    S.emit()
    return nc, S


def kernel(**inputs):
    inp = {k: np.asarray(v) for k, v in inputs.items()}
    wsrc = np.ascontiguousarray(build_wsrc(inp))
    consts = np.ascontiguousarray(build_consts(inp))
    x = inp["x"]
    p = inp["p"][0]
    B = x.shape[0]
    nc, _ = build_program()
    in_maps = [{"x": np.ascontiguousarray(x[b]), "p": np.ascontiguousarray(p[b]), "wsrc": wsrc, "consts": consts} for b in range(B)]
    res = run_bass_kernel_spmd(nc, in_maps, core_ids=list(range(B)))
    return np.stack([np.asarray(r["out"]) for r in res.results], axis=0).astype(np.float32)
```

```python
import numpy as np
import concourse.bass as bass
import concourse.mybir as mybir
from concourse.bass_utils import run_bass_kernel_spmd

F32 = mybir.dt.float32
BF16 = mybir.dt.bfloat16
I32 = mybir.dt.int32
AF = mybir.ActivationFunctionType
ALU = mybir.AluOpType

S_LEN = 4096
D = 1024
DFF = 2816
NJ = DFF // 128
DPLE = 256
T = 512
NT = S_LEN // T
KD = D // 128
EPS = 1e-6
NSLOT = 4
LOADW = 4096


class Buf:
    __slots__ = ("name", "w", "rs", "excl")

    def __init__(self, name, excl=False):
        self.name = name
        self.w = None
        self.rs = []
        self.excl = excl


class _Op:
    __slots__ = ("eng", "fn", "deps", "kind", "signal", "token")


class Sched:
    ENG = ("pe", "act", "dve", "pool", "sp")

    def __init__(self, nc, n_dma_sems=12):
        self.nc = nc
        self.h = {"pe": nc.tensor, "act": nc.scalar, "dve": nc.vector, "pool": nc.gpsimd, "sp": nc.sync}
        self.ops = []
        self.n_dma_sems = n_dma_sems

    def op(self, eng, fn, reads=(), writes=(), kind="c"):
        n = len(self.ops)
        deps = set()
        if any(r.excl for r in reads):
            writes = list(writes) + [r for r in reads if r.excl and r not in writes]
            reads = [r for r in reads if not r.excl]
        for r in reads:
            if r.w is not None:
                deps.add(r.w)
        for w in writes:
            if w.w is not None:
                deps.add(w.w)
            deps.update(w.rs)
        for r in reads:
            r.rs.append(n)
        for w in writes:
            w.w = n
            w.rs = []
        o = _Op()
        o.eng = eng
        o.fn = fn
        o.kind = kind
        o.signal = False
        o.token = None
        best = {}
        dl = []
        for d in deps:
            od = self.ops[d]
            if od.kind == "d":
                dl.append(d)
                continue
            if od.eng == "pe" and eng == "pe" and kind == "c":
                continue
            if od.eng not in best or best[od.eng] < d:
                best[od.eng] = d
        dl.extend(best.values())
        for d in dl:
            self.ops[d].signal = True
        o.deps = sorted(dl)
        self.ops.append(o)
        return n

    def dma(self, eng, fn, reads=(), writes=()):
        return self.op(eng, fn, reads, writes, kind="d")

    def emit(self):
        nc = self.nc
        esem = {e: nc.alloc_semaphore("prog_" + e) for e in ("pe", "act", "dve", "pool")}
        dsem = {e: [nc.alloc_semaphore(f"dma_{e}_{i}") for i in range(self.n_dma_sems)] for e in ("sp", "pool", "act")}
        dcnt = {e: [0] * self.n_dma_sems for e in ("sp", "pool", "act")}
        drr = {e: 0 for e in ("sp", "pool", "act")}
        cnt = {e: 0 for e in esem}
        seen = {e: {} for e in self.ENG}
        nwait = 0
        for o in self.ops:
            e = self.h[o.eng]
            sn = seen[o.eng]
            for d in o.deps:
                sem, val = self.ops[d].token
                key = id(sem)
                if sn.get(key, 0) < val:
                    e.wait_ge(sem, val)
                    sn[key] = val
                    nwait += 1
            if o.kind == "d":
                k = drr[o.eng] % self.n_dma_sems
                drr[o.eng] += 1
                sem = dsem[o.eng][k]
                prev = dcnt[o.eng][k]
                if prev > 0 and sn.get(id(sem), 0) < prev:
                    e.wait_ge(sem, prev)
                    sn[id(sem)] = prev
                    nwait += 1
                ins = o.fn(e)
                ins.then_inc(sem, 16)
                dcnt[o.eng][k] = prev + 16
                o.token = (sem, prev + 16)
            else:
                ins = o.fn(e)
                if o.signal:
                    cnt[o.eng] += 1
                    ins.then_inc(esem[o.eng], 1)
                    o.token = (esem[o.eng], cnt[o.eng])
        for q in dsem:
            for k in range(self.n_dma_sems):
                if dcnt[q][k] > 0 and seen[q].get(id(dsem[q][k]), 0) < dcnt[q][k]:
                    self.h[q].wait_ge(dsem[q][k], dcnt[q][k])
        self.stats = dict(n_ops=len(self.ops), n_wait=nwait, cnt=dict(cnt))

    def mm(self, out, lhsT, rhs, start, stop, reads, writes):
        return self.op("pe", lambda e: e.matmul(out, lhsT, rhs, start=start, stop=stop), reads, writes)

    def tr(self, out, in_, ident, reads, writes):
        return self.op("pe", lambda e: e.transpose(out, in_, ident), reads, writes)

    def act(self, out, in_, func, reads, writes, scale=None, bias=None, accum=None):
        kw = {}
        if scale is not None:
            kw["scale"] = scale
        if bias is not None:
            kw["bias"] = bias
        if accum is not None:
            kw["accum_out"] = accum
        return self.op("act", lambda e: e.activation(out=out, in_=in_, func=func, **kw), reads, writes)

    def tsc(self, eng, out, in0, s1, s2, op0, op1, reads, writes):
        if op1 is None:
            return self.op(eng, lambda e: e.tensor_scalar(out=out, in0=in0, scalar1=s1, scalar2=None, op0=op0), reads, writes)
        return self.op(eng, lambda e: e.tensor_scalar(out=out, in0=in0, scalar1=s1, scalar2=s2, op0=op0, op1=op1), reads, writes)

    def tt(self, eng, out, in0, in1, op, reads, writes):
        return self.op(eng, lambda e: e.tensor_tensor(out=out, in0=in0, in1=in1, op=op), reads, writes)

    def stt(self, out, in0, scalar, in1, op0, op1, reads, writes):
        return self.op("dve", lambda e: e.scalar_tensor_tensor(out=out, in0=in0, scalar=scalar, in1=in1, op0=op0, op1=op1), reads, writes)

    def copy(self, eng, out, in_, reads, writes):
        if eng == "act":
            return self.act(out, in_, AF.Copy, reads, writes)
        return self.op(eng, lambda e: e.tensor_copy(out=out, in_=in_), reads, writes)


class StatRing:
    NCOL = 64

    def __init__(self, nc, nsets=12):
        self.t = nc.alloc_sbuf_tensor("statring", [128, nsets * self.NCOL], F32).ap()
        self.nsets = nsets
        self.bufs = [[Buf(f"st{s}_{c}") for c in range(self.NCOL)] for s in range(nsets)]
        self.i = 0

    def new(self):
        s = self.i % self.nsets
        self.i += 1
        return _StatSet(self.t, s * self.NCOL, self.bufs[s])


class _StatSet:
    def __init__(self, t, base, bufs):
        self.t = t
        self.base = base
        self.bufs = bufs

    def col(self, c, n=1):
        return self.t[:, self.base + c:self.base + c + n]

    def b(self, c):
        return self.bufs[c]


NEWTON_STEPS = 2
INV_SQRT_D = 1.0 / 32.0


def rsqrt_chain(S, st, make_x, w=1):
    X, Y, Y2, TT = 8, 12, 16, 20
    x = st.col(X, w)
    y = st.col(Y, w)
    y2 = st.col(Y2, w)
    t = st.col(TT, w)
    make_x(x, st.b(X))
    S.tsc("dve", y.bitcast(I32), x.bitcast(I32), -0.5, float(0x5F3759DF), ALU.mult, ALU.add, [st.b(X)], [st.b(Y)])
    for _ in range(NEWTON_STEPS):
        S.tt("dve", y2, y, y, ALU.mult, [st.b(Y)], [st.b(Y2)])
        S.stt(t, y2, -0.5, x, ALU.mult, ALU.mult, [st.b(Y2), st.b(X)], [st.b(TT)])
        S.stt(y, t, 1.5, y, ALU.add, ALU.mult, [st.b(TT), st.b(Y)], [st.b(Y)])
    return y, st.b(Y)


def _lhsT_blocks(W):
    K, M = W.shape
    return W.reshape(K // 128, 128, M // 128, 128).transpose(1, 2, 0, 3)


def _rhs_blocks(W):
    K, N = W.shape
    return W.reshape(K // 128, 128, N // 512, 512).transpose(1, 2, 0, 3)


def stream_plan():
    blocks = []

    def ffn(tag, ph):
        for j in range(NJ):
            blocks.append((f"{tag}_gu{j}", 2048, ph))
        for nh in range(2):
            for j in range(NJ):
                blocks.append((f"{tag}_d{nh}_{j}", 512, ph))

    ffn("f1", 0)
    for nh in range(2):
        blocks.append((f"mc{nh}", 4096, 1))
    for nh in range(2):
        blocks.append((f"mv{nh}", 4096, 1))
    for mc in range(8):
        blocks.append((f"mu{mc}", 1024, 1))
    blocks.append(("poolw", 2048, 1))
    for mc in range(8):
        blocks.append((f"gated{mc}", 4096, 1))
    for nh in range(2):
        blocks.append((f"wo{nh}", 4096, 1))
    ffn("f2", 2)
    for nh in range(2):
        blocks.append((f"pleg{nh}", 4096, 3))
        blocks.append((f"plep{nh}", 1024, 3))
    loads = []
    where = {}
    off = 0
    cur = None
    for name, sz, ph in blocks:
        if cur is None or cur[1] + sz > LOADW or cur[2] != ph:
            cur = [off, 0, ph]
            loads.append(cur)
        where[name] = (len(loads) - 1, cur[1], sz)
        cur[1] += sz
        off += sz
    return [(n, z) for n, z, _ in blocks], loads, where, off


def build_wsrc(inp):
    parts = {}

    def ffn(tag, wg, wu, wd):
        g = _lhsT_blocks(wg)
        u = _lhsT_blocks(wu)
        for j in range(NJ):
            parts[f"{tag}_gu{j}"] = np.concatenate([g[:, j].reshape(128, 1024), u[:, j].reshape(128, 1024)], axis=1)
        d = wd.reshape(NJ, 128, 2, 512)
        for nh in range(2):
            for j in range(NJ):
                parts[f"{tag}_d{nh}_{j}"] = d[j, :, nh, :]

    ffn("f1", inp["ffn1_w_gate"][0], inp["ffn1_w_up"][0], inp["ffn1_w_down"][0])
    ffn("f2", inp["ffn2_w_gate"][0], inp["ffn2_w_up"][0], inp["ffn2_w_down"][0])
    w_in = inp["w_in"][0]
    v = _rhs_blocks(w_in[:, 1024:2048])
    c = _rhs_blocks(w_in[:, 2048:3072])
    for nh in range(2):
        parts[f"mv{nh}"] = v[:, nh].reshape(128, 4096)
        parts[f"mc{nh}"] = c[:, nh].reshape(128, 4096)
    u = _lhsT_blocks(w_in[:, 0:1024])
    ga = _lhsT_blocks(w_in[:, 3072:4096])
    gb = _lhsT_blocks(w_in[:, 4096:5120])
    oa = _lhsT_blocks(inp["w_out_a"][0])
    ob = _lhsT_blocks(inp["w_out_b"][0])
    for mc in range(8):
        parts[f"mu{mc}"] = u[:, mc].reshape(128, 1024)
        parts[f"gated{mc}"] = np.concatenate(
            [ga[:, mc].reshape(128, 1024), oa[:, mc].reshape(128, 1024), gb[:, mc].reshape(128, 1024), ob[:, mc].reshape(128, 1024)], axis=1)
    pw = inp["pool_w"][0]
    parts["poolw"] = np.concatenate([pw[g].reshape(2, 128, 256).transpose(1, 0, 2).reshape(128, 512) for g in range(4)], axis=1)
    wo = _rhs_blocks(inp["w_o"][0])
    pg = _rhs_blocks(inp["ple_w_gate"][0])
    pp = _rhs_blocks(inp["ple_w_proj"][0])
    for nh in range(2):
        parts[f"wo{nh}"] = wo[:, nh].reshape(128, 4096)
        parts[f"pleg{nh}"] = pg[:, nh].reshape(128, 4096)
        parts[f"plep{nh}"] = pp[:, nh].reshape(128, 1024)
    blocks, loads, where, tot = stream_plan()
    out = np.empty((128, tot), np.float32)
    off = 0
    for name, sz in blocks:
        a = parts[name]
        assert a.shape == (128, sz), (name, a.shape, sz)
        out[:, off:off + sz] = a
        off += sz
    return out


C_GB = 0
C_GCOL = 5120
C_PSCOL = 5152
C_PERS = 5160
C_WST = 5160
C_MASK = 5672
C_BROW = 5800
C_P = 6312
C_ID = 8360
C_END = 8488
POOL_WINDOWS = (2, 4, 8, 16)


def build_consts(inp):
    import ml_dtypes
    cst = np.zeros((128, C_END), np.float32)
    for i, k in enumerate(["ffn1_post_g", "mix_post_g", "ffn2_post_g", "ple_post_g", "sgu_norm_g"]):
        cst[:, C_GB + i * 1024:C_GB + (i + 1) * 1024] = inp[k][0][None, :]
    for i, k in enumerate(["ffn1_pre_g", "mix_pre_g", "ffn2_pre_g", "ple_pre_g"]):
        cst[:, C_GCOL + i * 8:C_GCOL + (i + 1) * 8] = inp[k][0].reshape(8, 128).T
    cst[:, C_PSCOL:C_PSCOL + 8] = inp["pool_scale"][0].reshape(8, 128).T
    for g in range(4):
        cst[:, C_WST + g * 128:C_WST + (g + 1) * 128] = inp["sgu_w"][0][g].T
        cst[:, C_BROW + g * 128:C_BROW + (g + 1) * 128] = inp["sgu_b"][0][g][None, :]
    s = np.arange(128)[:, None]
    t = np.arange(128)[None, :]
    cst[:, C_MASK:C_MASK + 128] = (s <= t).astype(np.float32)
    for wi, w in enumerate(POOL_WINDOWS):
        cur = np.where((t - s >= 0) & (t - s < w), 1.0 / w, 0.0) - (s == t)
        prev = np.where((t + 128 - s) < w, 1.0 / w, 0.0)
        cnt = np.minimum(t + 1, w).astype(np.float64)
        first = np.where((t - s >= 0) & (t - s < w), 1.0 / cnt, 0.0) - (s == t)
        hi = first.astype(np.float32).astype(ml_dtypes.bfloat16).astype(np.float32)
        lo = (first - hi).astype(np.float32).astype(ml_dtypes.bfloat16).astype(np.float32)
        cst[:, C_P + wi * 128:C_P + (wi + 1) * 128] = cur
        cst[:, C_P + (4 + wi) * 128:C_P + (5 + wi) * 128] = prev
        cst[:, C_P + (8 + wi) * 128:C_P + (9 + wi) * 128] = hi
        cst[:, C_P + (12 + wi) * 128:C_P + (13 + wi) * 128] = lo
    cst[:, C_ID:C_ID + 128] = np.eye(128, dtype=np.float32)
    return cst


class Ctx:
    pass


def build_program(n_tiles=NT, phases=4):
    nc = bass.Bass("TRN2", target_bir_lowering=False)
    S = Sched(nc)
    blocks, loads, where, WTOT = stream_plan()
    NL = len(loads)

    x_d = nc.dram_tensor("x", [S_LEN, D], F32, kind="ExternalInput").ap()
    p_d = nc.dram_tensor("p", [S_LEN, DPLE], F32, kind="ExternalInput").ap()
    w_d = nc.dram_tensor("wsrc", [128, WTOT], F32, kind="ExternalInput").ap()
    c_d = nc.dram_tensor("consts", [128, C_END], F32, kind="ExternalInput").ap()
    o_d = nc.dram_tensor("out", [S_LEN, D], F32, kind="ExternalOutput").ap()
    wbf_d = nc.dram_tensor("wbf", [128, WTOT], BF16, kind="Internal").ap()
    wbfB = [Buf(f"wbf{l}") for l in range(NL)]

    def sb(name, shape, dt):
        return nc.alloc_sbuf_tensor(name, shape, dt).ap()

    NH = 3
    h = [sb(f"h{i}", [128, 4, D], F32) for i in range(NH)]
    hB = [[Buf(f"h{i}_{c}") for c in range(4)] for i in range(NH)]
    slots = [sb(f"wslot{i}", [128, LOADW], BF16) for i in range(NSLOT)]
    slotB = [Buf(f"wslot{i}") for i in range(NSLOT)]
    xntok = sb("xntok", [128, 4, D], BF16)
    xntokB = [[Buf(f"xntok{c}_{hf}") for hf in range(2)] for c in range(4)]
    xnT = sb("xnT", [128, KD, T], BF16)
    xnTB = [Buf(f"xnT{k}") for k in range(KD)]
    NFM = 24
    fm = sb("fm", [128, NFM, T], BF16)
    fmB = [Buf(f"fm{i}") for i in range(NFM)]
    tmp = sb("tmp", [128, 4, D], F32)
    tmpB = [[Buf(f"tmp{c}_{nh}") for nh in range(2)] for c in range(4)]
    vn = sb("vn", [128, 4, D], BF16)
    vnB = [Buf(f"vn{c}") for c in range(4)]
    NCT = 6
    ctok = sb("ctok", [128, NCT, D], BF16)
    ctokB = [Buf(f"ctok{i}") for i in range(NCT)]
    NEW = 5
    ew = [sb(f"ew{i}", [128, T], F32) for i in range(NEW)]
    ewB = [Buf(f"ew{i}") for i in range(NEW)]
    junk = sb("junk", [128, D], BF16)
    junkB = Buf("junk")
    cpers = sb("cpers", [128, C_PERS], F32)
    cpersB = Buf("cpers_gb")
    ccolB = Buf("cpers_cols")
    wst = sb("wst", [128, 512], BF16)
    pmat = sb("pmat", [128, 16 * 128], BF16)
    ident = sb("ident", [128, 128], BF16)
    bhi = sb("bhi", [1, 512], BF16)
    blo = sb("blo", [1, 512], BF16)
    bres = sb("bres", [1, 512], F32)
    ones = sb("ones", [2, 128], BF16)
    bbc = sb("bbc", [128, 512], F32)
    cbfB = Buf("cbf")
    pbf = sb("pbf", [128, 4, DPLE], BF16)
    pbfB = Buf("pbf")
    pT = sb("pT", [128, 2 * T], BF16)
    pTB = Buf("pT")
    sr = StatRing(nc, nsets=12)

    ps = [nc.alloc_psum_tensor(f"ps{i}", [128, 512], F32).ap() for i in range(8)]
    psB = [Buf(f"ps{i}", excl=True) for i in range(8)]
    cx = Ctx()
    cx.bank = 0
    cx.ew = 0
    cx.flip = 0
    cx.pending = None

    def newbank():
        b = cx.bank % 8
        cx.bank += 1
        return b

    def newew():
        i = cx.ew % NEW
        cx.ew += 1
        return i

    def flip():
        cx.flip ^= 1
        return "act" if cx.flip else "dve"

    GB = lambda i: cpers[:, C_GB + i * 1024:C_GB + (i + 1) * 1024]
    GB_F1, GB_MIX, GB_F2, GB_PLE, GB_SGU = range(5)

    stage = tmp.rearrange("p a b -> p (a b)")
    allTmp = [b for row in tmpB for b in row]
    so = lambda off: off - C_PERS
    S.dma("sp", lambda e: e.dma_start(out=stage[:, so(C_ID):so(C_ID) + 128], in_=c_d[:, C_ID:C_ID + 128]), [], allTmp)
    S.dma("sp", lambda e: e.dma_start(out=cpers[:, C_GCOL:C_PERS], in_=c_d[:, C_GCOL:C_PERS]), [], [ccolB])
    S.copy("dve", ident[:, :], stage[:, so(C_ID):so(C_ID) + 128], allTmp, [cbfB])

    def load_bulk_consts():
        S.dma("sp", lambda e: e.dma_start(out=cpers[:, 0:C_GCOL], in_=c_d[:, 0:C_GCOL]), [hB[0][0]], [cpersB])
        S.dma("sp", lambda e: e.dma_start(out=stage[:, 0:so(C_ID)], in_=c_d[:, C_PERS:C_ID]), [hB[0][0]], allTmp)

    def setup_rest():
        for g in range(4):
            (lambda g: S.tt("dve", wst[:, g * 128:(g + 1) * 128], stage[:, so(C_WST) + g * 128:so(C_WST) + (g + 1) * 128],
                            stage[:, so(C_MASK):so(C_MASK) + 128], ALU.mult, allTmp, [cbfB]))(g)
        S.copy("dve", pmat[:, :], stage[:, so(C_P):so(C_P) + 2048], allTmp, [cbfB])
        S.copy("dve", bhi[0:1, :], stage[0:1, so(C_BROW):so(C_BROW) + 512], allTmp, [cbfB])
        S.tt("dve", bres[0:1, :], stage[0:1, so(C_BROW):so(C_BROW) + 512], bhi[0:1, :], ALU.subtract, allTmp + [cbfB], [cbfB])
        S.copy("dve", blo[0:1, :], bres[0:1, :], [cbfB], [cbfB])
        S.op("dve", lambda e: e.memset(ones[:, :], 1.0), [], [cbfB])
        for gi in (GB_F1, GB_F2):
            (lambda gi: S.tsc("dve", GB(gi), GB(gi), 0.5, None, ALU.mult, None, [cpersB], [cpersB]))(gi)

    laneA = [(t, ph) for t in range(0, n_tiles, 2) for ph in range(phases)]
    laneB = [(t, ph) for t in range(1, n_tiles, 2) for ph in range(phases)]
    nfill = min(3, phases)
    steps = []
    for k in range(nfill):
        steps.append(laneA[k])
        if k < len(laneB):
            steps.append(laneB[k])
    ia, ib = nfill, min(nfill, len(laneB))
    if ia < len(laneA):
        steps.append(laneA[ia])
        ia += 1
    while ia < len(laneA) or ib < len(laneB):
        if ia < len(laneA):
            steps.append(laneA[ia])
            ia += 1
        if ib < len(laneB):
            steps.append(laneB[ib])
            ib += 1
    gl_order = [(t, l) for (t, ph) in steps for l in range(NL) if loads[l][2] == ph]
    gl_pos = {tl: i for i, tl in enumerate(gl_order)}

    cx.next_load = 0
    cx.max_acc = -1

    def issue_loads(upto):
        while cx.next_load <= upto and cx.next_load < len(gl_order):
            G = cx.next_load
            t, l = gl_order[G]
            off, sz, _ = loads[l]
            s = G % NSLOT
            if t == 0:
                (lambda s, off, sz: S.dma("pool", lambda e: e.dma_start(out=slots[s][:, 0:sz], in_=w_d[:, off:off + sz],
                                                                      max_dma_last_dim=8192), [], [slotB[s]]))(s, off, sz)
                if n_tiles > 1:
                    (lambda s, off, sz, l: S.dma("sp", lambda e: e.dma_start(out=wbf_d[:, off:off + sz], in_=slots[s][:, 0:sz]),
                                                 [slotB[s]], [wbfB[l]]))(s, off, sz, l)
            else:
                (lambda s, off, sz, l: S.dma("sp", lambda e: e.dma_start(out=slots[s][:, 0:sz], in_=wbf_d[:, off:off + sz]),
                                             [wbfB[l]], [slotB[s]]))(s, off, sz, l)
            cx.next_load += 1

    def W(tile, name):
        l, o, sz = where[name]
        G = gl_pos[(tile, l)]
        assert G >= cx.max_acc - 1, (name, G, cx.max_acc)
        if G > cx.max_acc:
            cx.max_acc = G
            issue_loads(G + NSLOT - 2)
        s = G % NSLOT
        return slots[s][:, o:o + sz], slotB[s]

    def prenorm_squares(hp):
        st = sr.new()
        for c in range(4):
            S.act(junk[:, :], h[hp][:, c, :], AF.Square, [hB[hp][c]], [junkB, st.b(c)], accum=st.col(c), scale=INV_SQRT_D)
        return st

    def prenorm_stats(hp, defer=False):
        st = prenorm_squares(hp)
        if defer:
            cx.pending = lambda: prenorm_finish(hp, st)
            return
        prenorm_finish(hp, st)

    def prenorm_finish(hp, st):
        r, rB = rsqrt_chain(S, st, lambda x, xb: S.tsc("dve", x, st.col(0, 4), EPS, None, ALU.add, None,
                                                       [st.b(c) for c in range(4)], [xb]), w=4)
        for hf in range(2):
            cs = slice(hf * 512, (hf + 1) * 512)
            for c in range(4):
                if c == 3:
                    S.act(xntok[:, c, cs], h[hp][:, c, cs], AF.Copy, [hB[hp][c], rB], [xntokB[c][hf]], scale=r[:, c:c + 1])
                else:
                    S.tsc("dve", xntok[:, c, cs], h[hp][:, c, cs], r[:, c:c + 1], None, ALU.mult, None, [hB[hp][c], rB], [xntokB[c][hf]])

    def prenorm(hp, gci, stats_done=False):
        if not stats_done:
            prenorm_stats(hp)
        for kc in range(KD):
            b = newbank()
            pb = ps[b].bitcast(BF16)
            for c in range(4):
                S.tr(pb[:, c * 128:(c + 1) * 128], xntok[:, c, kc * 128:(kc + 1) * 128], ident[:, :],
                     [xntokB[c][kc // 4], cbfB], [psB[b]])
            gcol = cpers[:, C_GCOL + gci * 8 + kc:C_GCOL + gci * 8 + kc + 1]
            if flip() == "act":
                S.act(xnT[:, kc, :], pb[:, 0:T], AF.Copy, [psB[b], ccolB], [xnTB[kc]], scale=gcol)
            else:
                S.tsc("dve", xnT[:, kc, :], pb[:, 0:T], gcol, None, ALU.mult, None, [psB[b], ccolB], [xnTB[kc]])

    def postnorm_collect(b, c, nh, sts, gbi):
        S.act(junk[:, 0:T], ps[b][:, :], AF.Square, [psB[b]], [junkB, sts.b(nh * 4 + c)], accum=sts.col(nh * 4 + c), scale=INV_SQRT_D)
        S.tt("dve", tmp[:, c, nh * T:(nh + 1) * T], ps[b][:, :], GB(gbi)[:, nh * T:(nh + 1) * T], ALU.mult,
             [psB[b], cpersB], [tmpB[c][nh]])

    def residual_update(hp, st):
        r, rB = rsqrt_chain(S, st, lambda x, xb: S.stt(x, st.col(0, 4), EPS, st.col(4, 4), ALU.add, ALU.add,
                                                       [st.b(i) for i in range(8)], [xb]), w=4)
        for c in range(4):
            S.stt(h[hp][:, c, :], tmp[:, c, :], r[:, c:c + 1], h[hp][:, c, :], ALU.mult, ALU.add,
                  [tmpB[c][0], tmpB[c][1], rB, hB[hp][c]], [hB[hp][c]])

    def body_ffn(tile, tag, gbi, mid_hook=None):
        for j in range(NJ):
            wg, wB = W(tile, f"{tag}_gu{j}")
            bg = newbank()
            for kc in range(KD):
                S.mm(ps[bg][:, :], wg[:, kc * 128:(kc + 1) * 128], xnT[:, kc, :], kc == 0, kc == KD - 1, [wB, xnTB[kc]], [psB[bg]])
            bu = newbank()
            for kc in range(KD):
                S.mm(ps[bu][:, :], wg[:, 1024 + kc * 128:1024 + (kc + 1) * 128], xnT[:, kc, :], kc == 0, kc == KD - 1,
                     [wB, xnTB[kc]], [psB[bu]])
            e = newew()
            S.act(ew[e][:, :], ps[bg][:, :], AF.Silu, [psB[bg]], [ewB[e]])
            S.tt("dve", fm[:, j, :], ps[bu][:, :], ew[e][:, :], ALU.mult, [psB[bu], ewB[e]], [fmB[j]])
        if mid_hook is not None:
            mid_hook()
        sts = sr.new()
        for nh in range(2):
            banks = [newbank() for _ in range(4)]
            jt = NJ - 4 if nh == 1 else NJ
            for j in range(jt):
                wd, wB = W(tile, f"{tag}_d{nh}_{j}")
                for c in range(4):
                    S.mm(ps[banks[c]][:, :], fm[:, j, c * 128:(c + 1) * 128], wd, j == 0, j == NJ - 1, [fmB[j], wB], [psB[banks[c]]])
            for c in range(4):
                for j in range(jt, NJ):
                    wd, wB = W(tile, f"{tag}_d{nh}_{j}")
                    S.mm(ps[banks[c]][:, :], fm[:, j, c * 128:(c + 1) * 128], wd, j == 0, j == NJ - 1, [fmB[j], wB], [psB[banks[c]]])
                if jt < NJ:
                    postnorm_collect(banks[c], c, nh, sts, gbi)
            if jt == NJ:
                for c in range(4):
                    postnorm_collect(banks[c], c, nh, sts, gbi)
        return sts

    FU, FDIFF, FB, FY = 0, 8, 16, 8

    def bias_broadcast():
        bb = newbank()
        S.mm(ps[bb][:, :], ones[0:1, :], bhi[0:1, :], True, False, [cbfB], [psB[bb]])
        S.mm(ps[bb][:, :], ones[0:1, :], blo[0:1, :], False, True, [cbfB], [psB[bb]])
        S.copy("dve", bbc[:, :], ps[bb][:, :], [psB[bb]], [cbfB])

    cx.bias_done = False

    def body_mixer(tile):
        if not cx.bias_done:
            cx.bias_done = True
            bias_broadcast()
        cslot = lambda c: (tile * 4 + c) % NCT
        for nh in range(2):
            wc, wB = W(tile, f"mc{nh}")
            for c in range(4):
                b = newbank()
                for kc in range(KD):
                    S.mm(ps[b][:, :], xnT[:, kc, c * 128:(c + 1) * 128], wc[:, kc * T:(kc + 1) * T], kc == 0, kc == KD - 1,
                         [xnTB[kc], wB], [psB[b]])
                S.copy("act", ctok[:, cslot(c), nh * T:(nh + 1) * T], ps[b][:, :], [psB[b]], [ctokB[cslot(c)]])
        vst = sr.new()
        for nh in range(2):
            wv, wB = W(tile, f"mv{nh}")
            for c in range(4):
                b = newbank()
                for kc in range(KD):
                    S.mm(ps[b][:, :], xnT[:, kc, c * 128:(c + 1) * 128], wv[:, kc * T:(kc + 1) * T], kc == 0, kc == KD - 1,
                         [xnTB[kc], wB], [psB[b]])
                S.act(tmp[:, c, nh * T:(nh + 1) * T], ps[b][:, :], AF.Gelu_apprx_tanh, [psB[b]], [tmpB[c][nh]])
                (lambda c, nh: S.op("dve", lambda e: e.bn_stats(out=vst.col(c * 12 + nh * 6, 6), in_=tmp[:, c, nh * T:(nh + 1) * T]),
                                    [tmpB[c][nh]], [vst.b(c * 2 + nh)]))(c, nh)
        for c in range(4):
            (lambda c: S.op("dve", lambda e: e.bn_aggr(out=vst.col(48 + 2 * c, 2), in_=vst.col(c * 12, 12)),
                            [vst.b(c * 2), vst.b(c * 2 + 1)], [vst.b(48)]))(c)
        meanv = vst.t[:, vst.base + 48:vst.base + 56:2]
        varv = vst.t[:, vst.base + 49:vst.base + 56:2]
        r, rB = rsqrt_chain(S, sr.new(), lambda x, xb: S.tsc("dve", x, varv, EPS, None, ALU.add, None, [vst.b(48)], [xb]), w=4)
        S.stt(vst.col(56, 4), meanv, -1.0, r, ALU.mult, ALU.mult, [vst.b(48), rB], [vst.b(56)])
        for c in range(4):
            S.act(tmp[:, c, :], tmp[:, c, :], AF.Identity, [tmpB[c][0], tmpB[c][1], rB, vst.b(56)], [tmpB[c][0], tmpB[c][1]],
                  scale=r[:, c:c + 1], bias=vst.col(56 + c))
            S.tt("dve", vn[:, c, :], tmp[:, c, :], GB(GB_SGU), ALU.mult, [tmpB[c][0], tmpB[c][1], cpersB], [vnB[c]])
        for mc in range(8):
            wu, wB = W(tile, f"mu{mc}")
            b = newbank()
            for kc in range(KD):
                S.mm(ps[b][:, :], wu[:, kc * 128:(kc + 1) * 128], xnT[:, kc, :], kc == 0, kc == KD - 1, [wB, xnTB[kc]], [psB[b]])
            S.act(fm[:, FU + mc, :], ps[b][:, :], AF.Gelu_apprx_tanh, [psB[b]], [fmB[FU + mc]])
        for dc in range(8):
            g = dc // 2
            b = newbank()
            for c in range(4):
                o = ps[b][:, c * 128:(c + 1) * 128]
                S.mm(o, vn[:, c, dc * 128:(dc + 1) * 128], wst[:, g * 128:(g + 1) * 128], True, True, [vnB[c], cbfB], [psB[b]])
            e = newew()
            v3 = lambda ap: ap.rearrange("p (c t) -> p c t", c=4)
            bview = bbc[:, g * 128:(g + 1) * 128][:, None, :].broadcast_to([128, 4, 128])
            S.tt("dve", v3(ew[e][:, :]), v3(ps[b][:, :]), bview, ALU.add, [psB[b], cbfB], [ewB[e]])
            S.tt("dve", fm[:, FU + dc, :], ew[e][:, :], fm[:, FU + dc, :], ALU.mult, [ewB[e], fmB[FU + dc]], [fmB[FU + dc]])
        PM = lambda i: pmat[:, i * 128:(i + 1) * 128]
        for dc in range(8):
            wi = dc // 2
            b = newbank()
            for c in range(4):
                o = ps[b][:, c * 128:(c + 1) * 128]
                cur = ctok[:, cslot(c), dc * 128:(dc + 1) * 128]
                if tile == 0 and c == 0:
                    S.mm(o, cur, PM(8 + wi), True, False, [ctokB[cslot(c)], cbfB], [psB[b]])
                    S.mm(o, cur, PM(12 + wi), False, True, [ctokB[cslot(c)], cbfB], [psB[b]])
                else:
                    pslot = (tile * 4 + c - 1) % NCT
                    S.mm(o, cur, PM(wi), True, False, [ctokB[cslot(c)], cbfB], [psB[b]])
                    S.mm(o, ctok[:, pslot, dc * 128:(dc + 1) * 128], PM(4 + wi), False, True, [ctokB[pslot], cbfB], [psB[b]])
            S.copy(flip(), fm[:, FDIFF + dc, :], ps[b][:, :], [psB[b]], [fmB[FDIFF + dc]])
        wp, wB = W(tile, "poolw")
        for g in range(4):
            for oc in range(2):
                b = newbank()
                for k2 in range(2):
                    S.mm(ps[b][:, :], wp[:, g * 512 + k2 * 256 + oc * 128:g * 512 + k2 * 256 + (oc + 1) * 128], fm[:, FDIFF + 2 * g + k2, :],
                         k2 == 0, k2 == 1, [wB, fmB[FDIFF + 2 * g + k2]], [psB[b]])
                S.act(fm[:, FB + 2 * g + oc, :], ps[b][:, :], AF.Copy, [psB[b], ccolB], [fmB[FB + 2 * g + oc]],
                      scale=cpers[:, C_PSCOL + 2 * g + oc:C_PSCOL + 2 * g + oc + 1])
        for mc in range(8):
            wk, wB = W(tile, f"gated{mc}")
            res = []
            for half, src in ((0, FU), (1, FB)):
                bgt = newbank()
                for kc in range(KD):
                    S.mm(ps[bgt][:, :], wk[:, half * 2048 + kc * 128:half * 2048 + (kc + 1) * 128], xnT[:, kc, :], kc == 0, kc == KD - 1,
                         [wB, xnTB[kc]], [psB[bgt]])
                es = newew()
                S.act(ew[es][:, :], ps[bgt][:, :], AF.Sigmoid, [psB[bgt]], [ewB[es]])
                by = newbank()
                for kc in range(KD):
                    S.mm(ps[by][:, :], wk[:, half * 2048 + 1024 + kc * 128:half * 2048 + 1024 + (kc + 1) * 128], fm[:, src + kc, :],
                         kc == 0, kc == KD - 1, [wB, fmB[src + kc]], [psB[by]])
                S.tt("dve", ew[es][:, :], ps[by][:, :], ew[es][:, :], ALU.mult, [psB[by], ewB[es]], [ewB[es]])
                res.append(es)
            S.tt("dve", fm[:, FY + mc, :], ew[res[0]][:, :], ew[res[1]][:, :], ALU.add, [ewB[res[0]], ewB[res[1]]], [fmB[FY + mc]])
        sts = sr.new()
        for nh in range(2):
            wo, wB = W(tile, f"wo{nh}")
            for c in range(4):
                b = newbank()
                for kc in range(KD):
                    S.mm(ps[b][:, :], fm[:, FY + kc, c * 128:(c + 1) * 128], wo[:, kc * T:(kc + 1) * T], kc == 0, kc == KD - 1,
                         [fmB[FY + kc], wB], [psB[b]])
                postnorm_collect(b, c, nh, sts, GB_MIX)
        return sts

    def prep_p(tile):
        S.dma("pool", lambda e: e.dma_start(out=pbf[:, :, :], in_=p_d[tile * T:(tile + 1) * T, :].rearrange("(c q) d -> q c d", q=128)),
              [], [pbfB])
        b = newbank()
        pb = ps[b].bitcast(BF16)
        for k2 in range(2):
            for c in range(4):
                S.tr(pb[:, k2 * T + c * 128:k2 * T + (c + 1) * 128], pbf[:, c, k2 * 128:(k2 + 1) * 128], ident[:, :], [pbfB, cbfB], [psB[b]])
        S.copy("dve", pT[:, :], pb[:, 0:2 * T], [psB[b]], [pTB])

    def body_ple(tile):
        sts = sr.new()
        for nh in range(2):
            if nh == 1 and cx.pending is not None:
                fin, cx.pending = cx.pending, None
                fin()
            wg, wgB = W(tile, f"pleg{nh}")
            wp, wpB = W(tile, f"plep{nh}")
            for c in range(4):
                bg = newbank()
                for kc in range(KD):
                    S.mm(ps[bg][:, :], xnT[:, kc, c * 128:(c + 1) * 128], wg[:, kc * T:(kc + 1) * T], kc == 0, kc == KD - 1,
                         [xnTB[kc], wgB], [psB[bg]])
                be = newbank()
                for k2 in range(2):
                    S.mm(ps[be][:, :], pT[:, k2 * T + c * 128:k2 * T + (c + 1) * 128], wp[:, k2 * T:(k2 + 1) * T], k2 == 0, k2 == 1,
                         [pTB, wpB], [psB[be]])
                es = newew()
                S.act(ew[es][:, :], ps[bg][:, :], AF.Sigmoid, [psB[bg]], [ewB[es]])
                tv = tmp[:, c, nh * T:(nh + 1) * T]
                S.tt("dve", tv, ps[be][:, :], ew[es][:, :], ALU.mult, [psB[be], ewB[es]], [tmpB[c][nh]])
                S.act(junk[:, 0:T], tv, AF.Square, [tmpB[c][nh]], [junkB, sts.b(nh * 4 + c)], accum=sts.col(nh * 4 + c), scale=INV_SQRT_D)
                S.tt("dve", tv, tv, GB(GB_PLE)[:, nh * T:(nh + 1) * T], ALU.mult, [tmpB[c][nh], cpersB], [tmpB[c][nh]])
        return sts

    def load_x(tile, q="pool"):
        hp = tile % NH
        for c in range(4):
            r0 = tile * T + c * 128
            (lambda c, r0: S.dma(q, lambda e: e.dma_start(out=h[hp][:, c, :], in_=x_d[r0:r0 + 128, :]), [], [hB[hp][c]]))(c, r0)

    def store_h(tile):
        hp = tile % NH
        for c in range(4):
            r0 = tile * T + c * 128
            (lambda c, r0: S.dma("pool", lambda e: e.dma_start(out=o_d[r0:r0 + 128, :], in_=h[hp][:, c, :]), [hB[hp][c]], []))(c, r0)

    GCI = {0: 0, 1: 1, 2: 2, 3: 3}

    prep_at = {}
    last_ple = -1
    for q, (tq, phq) in enumerate(steps):
        if phq == 3:
            r = None
            for k in range(q - 1, last_ple, -1):
                if steps[k][1] in (0, 2):
                    r = k
                    break
            if r is not None:
                prep_at[r] = tq
            last_ple = q
    prepped = set()

    def BODY(st_, idx):
        t, ph = st_
        hooks = []
        if ph in (0, 2):
            if cx.first_hook:
                cx.first_hook = False
                hooks.append(first_hook)
            if idx == 1 and x2_late:
                hooks.append(lambda: load_x(2))
            if idx in prep_at and phases >= 4:
                tp = prep_at[idx]
                hooks.append(lambda: (prepped.add(tp), prep_p(tp)))
        hook = (lambda: [f() for f in hooks]) if hooks else None
        if ph == 0:
            return body_ffn(t, "f1", GB_F1, mid_hook=hook)
        if ph == 1:
            return body_mixer(t)
        if ph == 2:
            return body_ffn(t, "f2", GB_F2, mid_hook=hook)
        if t not in prepped:
            prepped.add(t)
            prep_p(t)
        return body_ple(t)

    def POST(st_, sts):
        t, ph = st_
        residual_update(t % NH, sts)
        if ph == phases - 1:
            store_h(t)
            if t + NH < n_tiles:
                load_x(t + NH)

    def PRE(st_, defer=False):
        prenorm_stats(st_[0] % NH, defer)

    def TR(st_):
        prenorm(0, GCI[st_[1]], stats_done=True)

    lane = lambda st_: st_[0] % 2
    load_x(0)
    issue_loads(NSLOT - 2)
    load_bulk_consts()
    if n_tiles > 1:
        load_x(1, q="sp")
    n = len(steps)
    cx.first_hook = True

    x2_late = n_tiles > 2 and len(steps) > 1 and steps[1][1] in (0, 2)

    def first_hook():
        setup_rest()
        if n > 1 and steps[1][0] != steps[0][0] and 1 not in pre_done:
            do_pre(1)
        if n_tiles > 2 and not x2_late:
            load_x(2)

    def prev_same_lane(i):
        for j in range(i - 1, -1, -1):
            if lane(steps[j]) == lane(steps[i]):
                return j
        return None

    pre_done = set()

    def do_pre(j):
        pre_done.add(j)
        defer = phases >= 4 and j >= 2 and steps[j - 1][1] == 3 and steps[j - 1][0] != steps[j][0]
        PRE(steps[j], defer)

    do_pre(0)
    TR(steps[0])
    for i in range(n):
        sts = BODY(steps[i], i)
        if i + 1 >= n:
            POST(steps[i], sts)
            break
        posted = False
        if lane(steps[i + 1]) == lane(steps[i]) and steps[i + 1][0] == steps[i][0]:
            POST(steps[i], sts)
            posted = True
        if (i + 1) not in pre_done:
            do_pre(i + 1)
        TR(steps[i + 1])
        if i + 2 < n and (i + 2) not in pre_done:
            p = prev_same_lane(i + 2)
            new_tile = p is None or steps[p][0] != steps[i + 2][0]
            if new_tile:
                do_pre(i + 2)
            elif p < i or (p == i and posted):
                do_pre(i + 2)
            elif p == i:
                POST(steps[i], sts)
                posted = True
                do_pre(i + 2)
        if not posted:
            POST(steps[i], sts)
    S.emit()
    return nc, S


def kernel(**inputs):
    inp = {k: np.asarray(v) for k, v in inputs.items()}
    wsrc = np.ascontiguousarray(build_wsrc(inp))
    consts = np.ascontiguousarray(build_consts(inp))
    x = inp["x"]
    p = inp["p"][0]
    B = x.shape[0]
    nc, _ = build_program()
    in_maps = [{"x": np.ascontiguousarray(x[b]), "p": np.ascontiguousarray(p[b]), "wsrc": wsrc, "consts": consts} for b in range(B)]
    res = run_bass_kernel_spmd(nc, in_maps, core_ids=list(range(B)))
    return np.stack([np.asarray(r["out"]) for r in res.results], axis=0).astype(np.float32)
```
